# Optimizing a Trainium2 kernel written in Bass

```python
import jax, jax.numpy as jnp
from jax import lax
import numpy as np

D_MODEL = 1024
BATCH = 8
SEQ = 4096
DEPTH = 2

CHUNK = 64
N_EVEN = (DEPTH + 1) // 2
N_ODD = DEPTH // 2
EPS = 1e-6

POOL_WINDOWS = (2, 4, 8, 16)
POOL_GROUP = D_MODEL // 8
POOL_WIDTH = POOL_GROUP * len(POOL_WINDOWS)
SC_WIDTH = D_MODEL // 2
SC_GROUPS = 8
CONV_WIDTH = 3
AB_IN = POOL_WIDTH + 3 * SC_WIDTH
AB_OUT = POOL_WIDTH + SC_WIDTH
GLA_HEADS = 4
GLA_DK = D_MODEL // 2
GLA_DV = D_MODEL
GLA_HK = GLA_DK // GLA_HEADS
GLA_HV = GLA_DV // GLA_HEADS
GLA_RANK = 16
GLA_TAU = 16.0
GLA_IN = 2 * GLA_DK + 2 * GLA_DV + GLA_RANK
D_FF = 2816

kernel_name = "hybrid_pool_shortconv_gla_convffn"


def rmsnorm(x, g):
    xf = x.astype(jnp.float32)
    y = xf * lax.rsqrt(jnp.mean(xf * xf, axis=-1, keepdims=True) + EPS)
    return (y * g.astype(jnp.float32)).astype(x.dtype)


def causal_dwconv(u, w, b):
    k = w.shape[0]
    s = u.shape[1]
    up = jnp.pad(u, ((0, 0), (k - 1, 0), (0, 0)))
    y = b
    for i in range(k):
        y = y + up[:, i:i + s] * w[i]
    return y


def pool_mixer(u, w, b, scale):
    bsz, s, _ = u.shape
    t = jnp.arange(1, s + 1, dtype=jnp.float32)[None, :, None]
    uf = u.astype(jnp.float32)
    outs = []
    for gi, win in enumerate(POOL_WINDOWS):
        ug = uf[..., gi * POOL_GROUP:(gi + 1) * POOL_GROUP]
        c = jnp.cumsum(ug, axis=1)
        c_lag = jnp.pad(c, ((0, 0), (win, 0), (0, 0)))[:, :s]
        mean = (c - c_lag) / jnp.minimum(t, float(win))
        outs.append(mean - ug)
    p = jnp.stack(outs, axis=2).astype(u.dtype)
    y = jnp.einsum('bsgc,gcd->bsgd', p, w).reshape(bsz, s, POOL_WIDTH) + b
    return y * scale


def gla_mixer(h, w_g2, b_g, norm_g):
    bsz, s, _ = h.shape
    f32 = jnp.float32
    q, k, v, r, gl = jnp.split(h, [GLA_DK, 2 * GLA_DK, 2 * GLA_DK + GLA_DV, 2 * GLA_DK + 2 * GLA_DV], axis=-1)
    g = jax.nn.log_sigmoid((gl @ w_g2 + b_g).astype(f32)) / GLA_TAU
    n = s // CHUNK

    def heads(t, hd):
        return t.astype(f32).reshape(bsz, n, CHUNK, GLA_HEADS, hd).transpose(0, 3, 1, 2, 4)

    q = heads(q, GLA_HK) * (GLA_HK ** -0.5)
    k = heads(k, GLA_HK)
    v = heads(v, GLA_HV)
    bcum = jnp.cumsum(heads(g, GLA_HK), axis=3)
    b_last = bcum[:, :, :, -1:]
    q_in = q * jnp.exp(bcum)
    k_in = k * jnp.exp(-bcum)
    mask = jnp.tril(jnp.ones((CHUNK, CHUNK), dtype=bool))
    att = jnp.where(mask, jnp.einsum('bhnlk,bhnmk->bhnlm', q_in, k_in), 0.0)
    o_intra = jnp.einsum('bhnlm,bhnmv->bhnlv', att, v)
    kv = jnp.einsum('bhnlk,bhnlv->bhnkv', k * jnp.exp(b_last - bcum), v)
    decay = jnp.exp(b_last[:, :, :, 0])

    def step(state, inp):
        kv_n, d_n = inp
        return d_n[..., None] * state + kv_n, state

    init = jnp.zeros((bsz, GLA_HEADS, GLA_HK, GLA_HV), f32)
    _, states = lax.scan(step, init, (jnp.moveaxis(kv, 2, 0), jnp.moveaxis(decay, 2, 0)))
    states = jnp.moveaxis(states, 0, 2)
    o = o_intra + jnp.einsum('bhnlk,bhnkv->bhnlv', q_in, states)
    o = o.transpose(0, 2, 3, 1, 4).reshape(bsz, s, GLA_HEADS, GLA_HV)
    o = o * lax.rsqrt(jnp.mean(o * o, axis=-1, keepdims=True) + EPS) * norm_g.astype(f32)
    o = o.reshape(bsz, s, GLA_DV) * jax.nn.silu(r.astype(f32))
    return o.astype(h.dtype)


def conv_ffn(h, w_up, conv_w, conv_b, w_down):
    u, v = jnp.split(h @ w_up, 2, axis=-1)
    u = causal_dwconv(u, conv_w, conv_b)
    return (jax.nn.gelu(u, approximate=False) * v) @ w_down


def setup_inputs(seed: int = 0) -> dict:
    key = jax.random.key(seed)
    ks = jax.random.split(key, 24)
    nrm = jax.random.normal
    f = jnp.float32
    return {
        "x": nrm(ks[0], (BATCH, SEQ, D_MODEL), f),
        "mix_norm": 1.0 + 0.02 * nrm(ks[1], (DEPTH, D_MODEL), f),
        "ffn_norm": 1.0 + 0.02 * nrm(ks[2], (DEPTH, D_MODEL), f),
        "ab_w_in": nrm(ks[3], (N_EVEN, D_MODEL, AB_IN), f) * D_MODEL ** -0.5,
        "pool_w": nrm(ks[4], (N_EVEN, len(POOL_WINDOWS), POOL_GROUP, POOL_GROUP), f) * POOL_GROUP ** -0.5,
        "pool_b": 0.02 * nrm(ks[5], (N_EVEN, POOL_WIDTH), f),
        "pool_scale": 1.0 + 0.02 * nrm(ks[6], (N_EVEN, POOL_WIDTH), f),
        "sc_conv_w": nrm(ks[7], (N_EVEN, CONV_WIDTH, SC_WIDTH), f) * CONV_WIDTH ** -0.5,
        "sc_conv_b": 0.02 * nrm(ks[8], (N_EVEN, SC_WIDTH), f),
        "ab_w_out": nrm(ks[9], (N_EVEN, AB_OUT, D_MODEL), f) * AB_OUT ** -0.5,
        "gla_w_in": nrm(ks[10], (N_ODD, D_MODEL, GLA_IN), f) * D_MODEL ** -0.5,
        "gla_w_g2": nrm(ks[11], (N_ODD, GLA_RANK, GLA_DK), f) * GLA_RANK ** -0.5,
        "gla_b_g": 0.1 * nrm(ks[12], (N_ODD, GLA_DK), f),
        "gla_norm": 1.0 + 0.02 * nrm(ks[13], (N_ODD, GLA_HV), f),
        "gla_w_out": nrm(ks[14], (N_ODD, GLA_DV, D_MODEL), f) * GLA_DV ** -0.5,
        "ffn_w_up": nrm(ks[15], (DEPTH, D_MODEL, 2 * D_FF), f) * D_MODEL ** -0.5,
        "ffn_conv_w": nrm(ks[16], (DEPTH, CONV_WIDTH, D_FF), f) * CONV_WIDTH ** -0.5,
        "ffn_conv_b": 0.02 * nrm(ks[17], (DEPTH, D_FF), f),
        "ffn_w_down": nrm(ks[18], (DEPTH, D_FF, D_MODEL), f) * D_FF ** -0.5,
        "final_norm": 1.0 + 0.02 * nrm(ks[19], (D_MODEL,), f),
    }


def reference(x, mix_norm, ffn_norm, ab_w_in, pool_w, pool_b, pool_scale, sc_conv_w, sc_conv_b,
              ab_w_out, gla_w_in, gla_w_g2, gla_b_g, gla_norm, gla_w_out, ffn_w_up, ffn_conv_w,
              ffn_conv_b, ffn_w_down, final_norm):
    for l in range(DEPTH):
        hn = rmsnorm(x, mix_norm[l])
        i = l // 2
        if l % 2 == 0:
            h = hn @ ab_w_in[i]
            pu, sb, sc, sx = jnp.split(h, [POOL_WIDTH, POOL_WIDTH + SC_WIDTH, POOL_WIDTH + 2 * SC_WIDTH], axis=-1)
            ya = pool_mixer(pu, pool_w[i], pool_b[i], pool_scale[i])
            yb = sb * causal_dwconv(sc * sx, sc_conv_w[i], sc_conv_b[i])
            y = jnp.concatenate([ya, yb], axis=-1) @ ab_w_out[i]
        else:
            y = gla_mixer(hn @ gla_w_in[i], gla_w_g2[i], gla_b_g[i], gla_norm[i]) @ gla_w_out[i]
        x = x + y
        x = x + conv_ffn(rmsnorm(x, ffn_norm[l]), ffn_w_up[l], ffn_conv_w[l], ffn_conv_b[l], ffn_w_down[l])
    return rmsnorm(x, final_norm)
```

```python
import numpy as np
from contextlib import ExitStack
import concourse.bass as bass
import concourse.mybir as mybir
from concourse.bass_utils import run_bass_kernel_spmd

F32 = mybir.dt.float32
BF16 = mybir.dt.bfloat16
AF = mybir.ActivationFunctionType
ALU = mybir.AluOpType

D = 1024
S = 4096
KC = 8
DFF = 2816
NCH = 22
SUB = 256
EPS = 1e-6
N_CORES = 8
FFN_SPLIT = (8, 7, 7)
SLOT_ELEMS = 25088
SEM_EPOCH = 12000


class Buf:
    __slots__ = ("name", "w", "r")

    def __init__(self, name):
        self.name = name
        self.w = None
        self.r = {}


class TK:
    def __init__(self, nc, es):
        self.nc = nc
        self.es = es
        self.eng = {"pe": nc.tensor, "act": nc.scalar, "dve": nc.vector, "pool": nc.gpsimd, "sp": nc.sync}
        self.sems = {}
        self.cnt = {}
        self.epoch = {k: 0 for k in self.eng}
        self.seen = {k: {} for k in self.eng}
        for k in self.eng:
            self._new_sem((k, 0))

    def _new_sem(self, key):
        self.sems[key] = self.es.enter_context(self.nc.semaphore("s_%s_%s" % key))
        self.cnt[key] = 0

    def dma_key(self, name):
        key = (name, 0)
        if key not in self.sems:
            self._new_sem(key)
        return key

    def _deps(self, e, reads, writes):
        deps = {}

        def add(st):
            if st is None:
                return
            k, v = st
            if deps.get(k, 0) < v:
                deps[k] = v

        for b in reads:
            add(b.w)
        for b in writes:
            add(b.w)
            for k, v in b.r.items():
                if k[0] == e:
                    continue
                add((k, v))
        return deps

    def _wait(self, e, deps):
        for k, v in deps.items():
            if k[0] == e and e == "pe":
                continue
            if self.seen[e].get(k, 0) >= v:
                continue
            self.eng[e].wait_ge(self.sems[k], v)
            self.seen[e][k] = v

    def op(self, e, fn, reads=(), writes=(), inc=True):
        self._wait(e, self._deps(e, reads, writes))
        ins = fn()
        key = (e, self.epoch[e])
        if inc:
            ins.then_inc(self.sems[key], 1)
            self.cnt[key] += 1
            stamp = (key, self.cnt[key])
            if self.cnt[key] >= SEM_EPOCH:
                self.epoch[e] += 1
                self._new_sem((e, self.epoch[e]))
        else:
            stamp = (key, self.cnt[key] + 1)
        for b in reads:
            if b.r.get(stamp[0], 0) < stamp[1]:
                b.r[stamp[0]] = stamp[1]
        for b in writes:
            b.w = stamp
            b.r = {}
        return ins

    def dma(self, q, name, out_ap, in_ap, reads=(), writes=()):
        key = self.dma_key(name)
        self._wait(q, self._deps(q, reads, writes))
        self.eng[q].dma_start(out=out_ap, in_=in_ap).then_inc(self.sems[key], 16)
        self.cnt[key] += 16
        stamp = (key, self.cnt[key])
        for b in reads:
            if b.r.get(key, 0) < stamp[1]:
                b.r[key] = stamp[1]
        for b in writes:
            b.w = stamp
            b.r = {}

    def wait_all(self, e, bufs):
        deps = {}
        for b in bufs:
            sts = [b.w] + list(b.r.items())
            for st in sts:
                if st is None:
                    continue
                k, v = st
                if deps.get(k, 0) < v:
                    deps[k] = v
        for k, v in deps.items():
            self.eng[e].wait_ge(self.sems[k], v)


class Rot:
    def __init__(self, items):
        self.items = items
        self.i = 0

    def next(self):
        it = self.items[self.i % len(self.items)]
        self.i += 1
        return it


def _cst_layout(depth):
    off = {}
    o = 0

    def add(name, n):
        nonlocal o
        off[name] = (o, n)
        o += n

    add("mixn", depth * 8)
    add("ffnn", depth * 8)
    add("finn", 8)
    add("pool_b", 4)
    add("pool_s", 4)
    add("sc_w", 12)
    add("sc_b", 4)
    add("fcw", depth * 3 * NCH)
    add("fcb", depth * NCH)
    add("gnorm", 2)
    add("eps", 1)
    add("invd", 64)
    add("bg", 512)
    add("trin", 128)
    add("revn", 128)
    add("maskt", 128)
    return off, o


def _pm(v, nchunk):
    return np.ascontiguousarray(np.asarray(v, np.float32).reshape(nchunk, 128).T)


def _wk(w):
    K, F = w.shape
    return np.ascontiguousarray(w.reshape(K // 128, 128, F).transpose(1, 0, 2).reshape(128, -1))


def prepare_host(inp):
    depth = 2
    off, ncst = _cst_layout(depth)
    cst = np.zeros((128, ncst), np.float32)

    def put(name, arr):
        o, n = off[name]
        assert arr.shape == (128, n), (name, arr.shape, n)
        cst[:, o:o + n] = arr

    put("mixn", np.concatenate([_pm(inp["mix_norm"][l], 8) for l in range(depth)], axis=1))
    put("ffnn", np.concatenate([_pm(inp["ffn_norm"][l], 8) for l in range(depth)], axis=1))
    put("finn", _pm(inp["final_norm"], 8))
    put("pool_b", _pm(inp["pool_b"][0], 4))
    put("pool_s", _pm(inp["pool_scale"][0], 4))
    put("sc_w", np.concatenate([_pm(inp["sc_conv_w"][0][i], 4) for i in range(3)], axis=1))
    put("sc_b", _pm(inp["sc_conv_b"][0], 4))
    put("fcw", np.concatenate([_pm(inp["ffn_conv_w"][l][i], NCH) for l in range(depth) for i in range(3)], axis=1))
    put("fcb", np.concatenate([_pm(inp["ffn_conv_b"][l], NCH) for l in range(depth)], axis=1))
    put("gnorm", _pm(inp["gla_norm"][0], 2))
    put("eps", np.full((128, 1), EPS, np.float32))
    invd = np.zeros((128, 64), np.float32)
    for gi, win in enumerate((2, 4, 8, 16)):
        t = np.arange(1, 17, dtype=np.float32)
        invd[:, gi * 16:(gi + 1) * 16] = 1.0 / np.minimum(t, float(win))
    put("invd", invd)
    put("bg", np.ascontiguousarray(np.broadcast_to(np.asarray(inp["gla_b_g"][0], np.float32)[None, :], (128, 512))))
    s_i = np.arange(128)[:, None]
    t_i = np.arange(128)[None, :]
    put("trin", np.where(s_i <= t_i, -1.0 / 16.0, 0.0).astype(np.float32))
    put("revn", np.where(s_i > t_i, -1.0 / 16.0, 0.0).astype(np.float32))
    put("maskt", np.where(s_i <= t_i, 1.0, 0.0).astype(np.float32))

    wts = {}
    wts["w_m0"] = np.concatenate([
        _wk(np.asarray(inp["ab_w_in"][0], np.float32)),
        np.ascontiguousarray(np.asarray(inp["pool_w"][0], np.float32).transpose(1, 0, 2).reshape(128, 512)),
        _wk(np.asarray(inp["ab_w_out"][0], np.float32)),
    ], axis=1)
    for l in range(depth):
        wu = np.asarray(inp["ffn_w_up"][l], np.float32)
        wd = np.asarray(inp["ffn_w_down"][l], np.float32)
        c0 = 0
        for i, n in enumerate(FFN_SPLIT):
            u = _wk(wu[:, c0 * 128:(c0 + n) * 128])
            v = _wk(wu[:, DFF + c0 * 128:DFF + (c0 + n) * 128])
            dn = _wk(wd[c0 * 128:(c0 + n) * 128, :])
            wts["w_f%d_%d" % (l, i)] = np.concatenate([u, v, dn], axis=1)
            c0 += n
    wts["w_gin"] = _wk(np.asarray(inp["gla_w_in"][0], np.float32))
    wts["w_gout"] = _wk(np.asarray(inp["gla_w_out"][0], np.float32))
    wts["wg2"] = np.ascontiguousarray(np.asarray(inp["gla_w_g2"][0], np.float32))
    wts["cst"] = cst
    return wts


def build_program(layers=(0, 1), do_final=True, G=1024, n_groups=None, debug_stage=None):
    depth = 2
    NSUB = G // SUB
    NG = S // G if n_groups is None else n_groups
    coff, ncst = _cst_layout(depth)
    nc = bass.Bass("TRN2", target_bir_lowering=False)
    es = ExitStack()
    tk = TK(nc, es)

    def dram(name, shape, kind):
        return nc.dram_tensor(name, list(shape), F32, kind=kind).ap()

    xT = dram("xT", [D, S], "ExternalInput")
    yT = dram("yT", [D, S], "ExternalOutput")
    cst_d = dram("cst", [128, ncst], "ExternalInput")
    wg2_d = dram("wg2", [16, 512], "ExternalInput")
    piece_cols = {"w_m0": 25088, "w_gin": 24704, "w_gout": 8192}
    for l in range(depth):
        for i, n in enumerate(FFN_SPLIT):
            piece_cols["w_f%d_%d" % (l, i)] = 3072 * n
    piece_d = {}
    need = set()
    for l in layers:
        if l % 2 == 0:
            need.add("w_m0")
        else:
            need.update(["w_gin", "w_gout"])
        for i in range(len(FFN_SPLIT)):
            need.add("w_f%d_%d" % (l, i))
    for name in sorted(need):
        piece_d[name] = dram(name, [128, piece_cols[name]], "ExternalInput")

    def sb(name, shape, dt):
        return es.enter_context(nc.sbuf_tensor(name, list(shape), dt))

    xres = sb("xres", [128, KC, G], F32)
    hn = sb("hn", [128, KC, G + 2], BF16)
    slots = [sb("slotA", [128, SLOT_ELEMS], BF16), sb("slotB", [128, SLOT_ELEMS], BF16)]
    cst = sb("cst_sb", [128, ncst], F32)
    ones_bf = sb("ones_bf", [128, 128], BF16)
    halo_save = sb("halo_save", [128, depth, KC, 2], BF16)
    sq_t = [sb("sq%d" % i, [128, SUB], BF16) for i in range(2)]
    ln_t = sb("ln_t", [128, SUB], F32)
    rstd_t = [sb("rstd%d" % i, [128, SUB], F32) for i in range(2)]
    ft_t = [sb("ft%d" % i, [128, SUB], F32) for i in range(3)]
    ge_t = [sb("ge%d" % i, [128, SUB], F32) for i in range(2)]
    g_t = [sb("g%d" % i, [128, SUB], BF16) for i in range(3)]
    pbuf = sb("pbuf", [128, 4, 272], F32)
    zbuf = sb("zbuf", [128, 4, 258], F32)
    pt_t = [sb("pt%d" % i, [128, 272], F32) for i in range(2)]
    pfix_t = sb("pfix", [128, 16], F32)
    pbf_t = [sb("pbf%d" % i, [128, SUB], BF16) for i in range(2)]
    yab = sb("yab", [128, KC, SUB], BF16)
    sx_t = [sb("sx%d" % i, [128, SUB], F32) for i in range(2)]
    ct_t = [sb("ct%d" % i, [128, SUB], F32) for i in range(2)]

    has_gla = any(l % 2 == 1 for l in layers)
    if has_gla:
        wg2_sb = sb("wg2_sb", [16, 512], BF16)
        gl_sb = sb("gl_sb", [16, SUB], BF16)
        gt_t = sb("gt_t", [128, 512], F32)
        eq_t = sb("eq_t", [128, 512], F32)
        ek_t = sb("ek_t", [128, 512], F32)
        qin_t = sb("qin_t", [128, 4, 128], BF16)
        kin_t = sb("kin_t", [128, 4, 128], BF16)
        kdec_t = sb("kdec_t", [128, 512], BF16)
        vsb_t = sb("vsb_t", [128, 1024], BF16)
        att_t = [sb("att%d" % i, [128, 128], BF16) for i in range(2)]
        S_t = sb("S_t", [128, 4, 256], F32)
        Sbf_t = sb("Sbf_t", [128, 4, 256], BF16)
        osq_t = [sb("osq%d" % i, [128, 256], BF16) for i in range(2)]
        rsh_t = [sb("rsh%d" % i, [128, 128], F32) for i in range(2)]
        sr_t = sb("sr_t", [128, KC, SUB], BF16)

    banks = [es.enter_context(nc.psum_tensor("bank%d" % i, [128, 512], F32)) for i in range(8)]

    bankbuf = [Buf("bank%d" % i) for i in range(8)]

    class PT:
        def __init__(self, bi, c0, n):
            self.bank, self.c0, self.n = banks[bi], c0, n
            self.buf = bankbuf[bi]

        def ap(self, a=0, b=None, p0=0, p1=128):
            b = self.n if b is None else b
            return self.bank[p0:p1, self.c0 + a:self.c0 + b]

    H = [PT(i // 2, (i % 2) * 256, 256) for i in range(8)]
    U = [PT(4, 0, 512), PT(5, 0, 512)]
    V = [PT(6, 0, 256), PT(7, 0, 256)]
    half_rot = Rot([PT(b, 0, 256) for b in range(8)])
    m_rot = Rot([PT(6, 256, 256), PT(7, 256, 256)])
    full_rot = Rot([PT(b, 0, 512) for b in range(8)])

    xb = [[Buf("x%d_%d" % (c, j)) for j in range(NSUB)] for c in range(KC)]
    hnb = [Buf("hn%d" % j) for j in range(NSUB)]
    hnhalo = Buf("hnhalo")
    slotb = [Buf("slotA"), Buf("slotB")]
    cstb = Buf("cst")
    onesb = Buf("ones")
    halob = [Buf("halo_save%d" % l) for l in range(depth)]
    sq_rot = Rot([(t, Buf("sq")) for t in sq_t])
    lnb = Buf("ln")
    rstd_rot = Rot([(t, Buf("rstd")) for t in rstd_t])
    ft_rot = Rot([(t, Buf("ft")) for t in ft_t])
    ge_rot = Rot([(t, Buf("ge")) for t in ge_t])
    g_rot = Rot([(t, Buf("g")) for t in g_t])
    pbufb = [Buf("pbuf%d" % i) for i in range(4)]
    zbufb = [Buf("zbuf%d" % i) for i in range(4)]
    pt_rot = Rot([(t, Buf("pt")) for t in pt_t])
    pfixb = Buf("pfix")
    pbf_rot = Rot([(t, Buf("pbf")) for t in pbf_t])
    yabb = [Buf("yab%d" % i) for i in range(KC)]
    sx_rot = Rot([(t, Buf("sx")) for t in sx_t])
    ct_rot = Rot([(t, Buf("ct")) for t in ct_t])

    if has_gla:
        wg2b, glb, gtb, eqb, ekb = (Buf(n) for n in ("wg2", "gl", "gt", "eq", "ek"))
        qinb = [Buf("qin%d" % h) for h in range(4)]
        kinb = [Buf("kin%d" % h) for h in range(4)]
        kdecb, vsbb = Buf("kdec"), Buf("vsb")
        att_rot = Rot([(t, Buf("att")) for t in att_t])
        Sb = [Buf("S%d" % h) for h in range(4)]
        Sbfb = [Buf("Sbf%d" % h) for h in range(4)]
        osq_rot = Rot([(t, Buf("osq")) for t in osq_t])
        rsh_rot = Rot([(t, Buf("rsh")) for t in rsh_t])
        srb = [Buf("sr%d" % c) for c in range(KC)]

    def C(name, a=0, b=None):
        o, n = coff[name]
        b = n if b is None else b
        return cst[:, o + a:o + b]

    op = tk.op
    act, dve, pe = nc.scalar, nc.vector, nc.tensor

    tk.dma("sp", "cst", cst[:], cst_d, writes=[cstb])
    op("dve", lambda: dve.memset(ones_bf[:], 1.0), writes=[onesb])
    op("dve", lambda: dve.memset(halo_save[:], 0.0), writes=halob)
    op("dve", lambda: dve.memset(pbuf[:], 0.0), writes=pbufb)
    op("dve", lambda: dve.memset(zbuf[:], 0.0), writes=zbufb)

    if has_gla:
        tk.dma("pool", "wg2", wg2_sb[:], wg2_d, writes=[wg2b])
        op("dve", lambda: dve.memset(S_t[:], 0.0), writes=Sb)
        op("dve", lambda: dve.memset(Sbf_t[:], 0.0), writes=Sbfb)

    piece_seq = []
    for g in range(NG):
        for l in layers:
            if l % 2 == 0:
                piece_seq.append("w_m0")
            else:
                piece_seq += ["w_gin", "w_gout"]
            for i in range(len(FFN_SPLIT)):
                piece_seq.append("w_f%d_%d" % (l, i))
    wstate = {"next": 0}

    def issue_piece():
        i = wstate["next"]
        if i >= len(piece_seq):
            return
        wstate["next"] = i + 1
        name = piece_seq[i]
        s = i % 2
        ncols = piece_cols[name]
        c0 = 0
        while c0 < ncols:
            c1 = min(ncols, c0 + 4096)
            tk.dma("pool", "w%d" % s, slots[s][:, c0:c1], piece_d[name][:, c0:c1], writes=[slotb[s]])
            c0 = c1

    cur_piece = {"i": -1}

    def begin_piece(name, prefetch=True):
        cur_piece["i"] += 1
        i = cur_piece["i"]
        assert piece_seq[i] == name, (piece_seq[i], name)
        lim = i + 1 if prefetch else i
        while wstate["next"] <= lim and wstate["next"] < len(piece_seq):
            issue_piece()
        return i % 2

    def prefetch_after_current():
        while wstate["next"] <= cur_piece["i"] + 1 and wstate["next"] < len(piece_seq):
            issue_piece()

    def tok(j):
        return slice(j * SUB, (j + 1) * SUB)

    def norm_stats(j):
        ms = m_rot.next()
        for c in range(KC):
            sq, sqb = sq_rot.next()
            if c % 2 == 0:
                op("act", lambda: act.activation(sq[:], xres[:, c, tok(j)], AF.Square),
                   reads=[xb[c][j]], writes=[sqb])
            else:
                op("dve", lambda: dve.tensor_tensor(sq[:], xres[:, c, tok(j)], xres[:, c, tok(j)], op=ALU.mult),
                   reads=[xb[c][j]], writes=[sqb])
            op("pe", lambda: pe.matmul(ms.ap(), ones_bf[:], sq[:], start=(c == 0), stop=(c == KC - 1)),
               reads=[onesb, sqb], writes=[ms.buf], inc=True)
        op("act", lambda: act.activation(ln_t[:], ms.ap(), AF.Ln, bias=C("eps"), scale=1.0 / D),
           reads=[ms.buf, cstb], writes=[lnb])
        rstd, rstdb = rstd_rot.next()
        op("act", lambda: act.activation(rstd[:], ln_t[:], AF.Exp, scale=-0.5), reads=[lnb], writes=[rstdb])
        return rstd, rstdb

    def norm_to_hn(j, gname, l):
        rstd, rstdb = norm_stats(j)
        for c in range(KC):
            gcol = C(gname, l * 8 + c, l * 8 + c + 1)
            op("dve", lambda: dve.scalar_tensor_tensor(out=hn[:, c, 2 + j * SUB:2 + (j + 1) * SUB],
                                                        in0=xres[:, c, tok(j)], scalar=gcol, in1=rstd[:],
                                                        op0=ALU.mult, op1=ALU.mult),
               reads=[xb[c][j], rstdb, cstb], writes=[hnb[j]])

    def resid_add(j, m, pt):
        op("dve", lambda: dve.tensor_tensor(xres[:, m, tok(j)], xres[:, m, tok(j)], pt.ap(), op=ALU.add),
           reads=[xb[m][j], pt.buf], writes=[xb[m][j]])

    def mixer0(j, s, first_tile, part=None):
        W = slots[s]
        wb = slotb[s]
        hsl = slice(2 + j * SUB, 2 + (j + 1) * SUB)

        def proj(col0):
            pt = half_rot.next()
            for kc in range(KC):
                op("pe", lambda: pe.matmul(pt.ap(), W[:, kc * 2048 + col0:kc * 2048 + col0 + 128], hn[:, kc, hsl],
                                           start=(kc == 0), stop=(kc == KC - 1)),
                   reads=[wb, hnb[j]], writes=[pt.buf], inc=(kc == KC - 1))
            return pt

        if part == "proj":
            proj(0)
            return
        def pool_chain(gi):
            win = 2 << gi
            L = gi + 1
            pu = proj(gi * 128)
            yield
            op("act", lambda: act.copy(pbuf[:, gi, 16:272], pu.ap()), reads=[pu.buf], writes=[pbufb[gi]])
            yield
            hist = [0] * (L + 1)
            for k in range(L, 0, -1):
                hist[k - 1] = hist[k] + (1 << (k - 1))
            prev = (lambda a, b, gi=gi, h0=hist[0]: pbuf[:, gi, 16 - h0 + a:16 - h0 + b])
            prevb = pbufb[gi]
            for k in range(1, L + 1):
                sh = 1 << (k - 1)
                n = SUB + hist[k]
                cur, curb = pt_rot.next()
                op("dve", lambda: dve.tensor_tensor(cur[:, 0:n], prev(sh, sh + n), prev(0, n), op=ALU.add),
                   reads=[prevb], writes=[curb])
                prev = (lambda a, b, cur=cur: cur[:, a:b])
                prevb = curb
                yield
            pbf, pbfb = pbf_rot.next()
            op("dve", lambda: dve.scalar_tensor_tensor(out=pbf[:], in0=prev(0, SUB), scalar=1.0 / win,
                                                        in1=pbuf[:, gi, 16:272], op0=ALU.mult, op1=ALU.subtract),
               reads=[prevb, pbufb[gi]], writes=[pbfb])
            yield
            if first_tile:
                op("dve", lambda: dve.tensor_tensor(pfix_t[:], prev(0, 16), C("invd", gi * 16, gi * 16 + 16), op=ALU.mult),
                   reads=[prevb, cstb], writes=[pfixb])
                op("dve", lambda: dve.tensor_tensor(pbf[:, 0:16], pfix_t[:], pbuf[:, gi, 16:32], op=ALU.subtract),
                   reads=[pfixb, pbufb[gi], pbfb], writes=[pbfb])
                yield
            op("act", lambda: act.copy(pbuf[:, gi, 0:16], pbuf[:, gi, 256:272]), reads=[pbufb[gi]], writes=[pbufb[gi]])
            ya = half_rot.next()
            op("pe", lambda: pe.matmul(ya.ap(), W[:, 16384 + gi * 128:16384 + (gi + 1) * 128], pbf[:],
                                       start=True, stop=True), reads=[wb, pbfb], writes=[ya.buf])
            yield
            op("dve", lambda: dve.tensor_scalar(yab[:, gi, :], ya.ap(), C("pool_b", gi, gi + 1), C("pool_s", gi, gi + 1),
                                                op0=ALU.add, op1=ALU.mult),
               reads=[ya.buf, cstb], writes=[yabb[gi]])

        def conv_chain(ci):
            sbp = proj(512 + ci * 128)
            yield
            scp = proj(1024 + ci * 128)
            yield
            sxp = proj(1536 + ci * 128)
            yield
            sx, sxb = sx_rot.next()
            op("act", lambda: act.copy(sx[:], sxp.ap()), reads=[sxp.buf], writes=[sxb])
            yield
            op("dve", lambda: dve.tensor_tensor(zbuf[:, ci, 2:258], scp.ap(), sx[:], op=ALU.mult),
               reads=[scp.buf, sxb], writes=[zbufb[ci]])
            yield
            t1, t1b = ct_rot.next()
            op("act", lambda: act.activation(t1[:], zbuf[:, ci, 2:258], AF.Identity,
                                             bias=C("sc_b", ci, ci + 1), scale=C("sc_w", 8 + ci, 9 + ci)),
               reads=[zbufb[ci], cstb], writes=[t1b])
            yield
            t2, t2b = ct_rot.next()
            op("dve", lambda: dve.scalar_tensor_tensor(out=t2[:], in0=zbuf[:, ci, 1:257], scalar=C("sc_w", 4 + ci, 5 + ci),
                                                        in1=t1[:], op0=ALU.mult, op1=ALU.add),
               reads=[zbufb[ci], t1b, cstb], writes=[t2b])
            yield
            op("dve", lambda: dve.scalar_tensor_tensor(out=t1[:], in0=zbuf[:, ci, 0:256], scalar=C("sc_w", ci, ci + 1),
                                                        in1=t2[:], op0=ALU.mult, op1=ALU.add),
               reads=[zbufb[ci], t2b, cstb], writes=[t1b])
            yield
            op("dve", lambda: dve.tensor_tensor(yab[:, 4 + ci, :], t1[:], sbp.ap(), op=ALU.mult),
               reads=[t1b, sbp.buf], writes=[yabb[4 + ci]])
            op("act", lambda: act.copy(zbuf[:, ci, 0:2], zbuf[:, ci, 256:258]), reads=[zbufb[ci]], writes=[zbufb[ci]])

        for i in range(4):
            gens = [pool_chain(i), conv_chain(i)]
            while gens:
                for gch in list(gens):
                    try:
                        next(gch)
                    except StopIteration:
                        gens.remove(gch)
        for m in range(KC):
            pt = half_rot.next()
            for kc in range(KC):
                op("pe", lambda: pe.matmul(pt.ap(), W[:, 16896 + kc * 1024 + m * 128:16896 + kc * 1024 + (m + 1) * 128],
                                           yab[:, kc, :], start=(kc == 0), stop=(kc == KC - 1)),
                   reads=[wb, yabb[kc]], writes=[pt.buf], inc=(kc == KC - 1))
            resid_add(j, m, pt)

    def mixer1(j, s_in, s_out, last=False):
        W, wb = slots[s_in], slotb[s_in]
        Wo, wob = slots[s_out], slotb[s_out]
        GI = 3088
        hsl = slice(2 + j * SUB, 2 + (j + 1) * SUB)
        QS = 128.0 ** -0.5

        def proj_fm(col0, mrows=128):
            pt = full_rot.next()
            for kc in range(KC):
                op("pe", lambda: pe.matmul(pt.ap(0, SUB, 0, mrows), W[:, kc * GI + col0:kc * GI + col0 + mrows], hn[:, kc, hsl],
                                           start=(kc == 0), stop=(kc == KC - 1)),
                   reads=[wb, hnb[j]], writes=[pt.buf], inc=(kc == KC - 1))
            return pt

        glp = proj_fm(3072, 16)
        op("act", lambda: act.copy(gl_sb[:], glp.ap(0, SUB, 0, 16)), reads=[glp.buf], writes=[glb])
        for c in range(KC):
            rp = proj_fm(2048 + c * 128)
            op("act", lambda: act.activation(sr_t[:, c, :], rp.ap(0, SUB), AF.Silu), reads=[rp.buf], writes=[srb[c]])

        for tt in range(2):
            ts = slice(tt * 128, (tt + 1) * 128)
            hts = slice(2 + j * SUB + tt * 128, 2 + j * SUB + (tt + 1) * 128)
            zp = full_rot.next()
            op("pe", lambda: pe.matmul(zp.ap(), gl_sb[:, ts], wg2_sb[:], start=True, stop=True),
               reads=[glb, wg2b], writes=[zp.buf])
            vtps = []
            for half in range(2):
                vtp = full_rot.next()
                for kc in range(KC):
                    op("pe", lambda: pe.matmul(vtp.ap(), hn[:, kc, hts],
                                               W[:, kc * GI + 1024 + half * 512:kc * GI + 1024 + (half + 1) * 512],
                                               start=(kc == 0), stop=(kc == KC - 1)),
                       reads=[wb, hnb[j]], writes=[vtp.buf], inc=(kc == KC - 1))
                vtps.append(vtp)
            op("dve", lambda: dve.tensor_tensor(gt_t[:], zp.ap(), C("bg"), op=ALU.add), reads=[zp.buf, cstb], writes=[gtb])
            op("act", lambda: act.activation(gt_t[:], gt_t[:], AF.Exp, scale=-1.0), reads=[gtb], writes=[gtb])
            op("act", lambda: act.activation(gt_t[:], gt_t[:], AF.Ln, bias=1.0), reads=[gtb], writes=[gtb])
            bc = full_rot.next()
            for h in range(4):
                op("pe", lambda: pe.matmul(bc.ap(h * 128, (h + 1) * 128), gt_t[:, h * 128:(h + 1) * 128], C("trin"),
                                           start=True, stop=True, skip_group_check=True),
                   reads=[gtb, cstb], writes=[bc.buf], inc=(h == 3))
            rv = full_rot.next()
            op("pe", lambda: pe.matmul(rv.ap(), C("revn"), gt_t[:], start=True, stop=True), reads=[gtb, cstb], writes=[rv.buf])
            op("act", lambda: act.activation(eq_t[:], bc.ap(), AF.Exp), reads=[bc.buf], writes=[eqb])
            op("act", lambda: act.activation(ek_t[:], bc.ap(), AF.Exp, scale=-1.0), reads=[bc.buf], writes=[ekb])
            op("act", lambda: act.activation(gt_t[:], rv.ap(), AF.Exp), reads=[rv.buf], writes=[gtb])
            for half in range(2):
                op("act", lambda: act.copy(vsb_t[:, half * 512:(half + 1) * 512], vtps[half].ap()),
                   reads=[vtps[half].buf], writes=[vsbb])
            for h in range(4):
                qp = full_rot.next()
                for kc in range(KC):
                    op("pe", lambda: pe.matmul(qp.ap(0, 128), W[:, kc * GI + h * 128:kc * GI + (h + 1) * 128], hn[:, kc, hts],
                                               start=(kc == 0), stop=(kc == KC - 1)),
                       reads=[wb, hnb[j]], writes=[qp.buf], inc=(kc == KC - 1))
                op("dve", lambda: dve.scalar_tensor_tensor(out=qin_t[:, h, :], in0=qp.ap(0, 128), scalar=QS,
                                                            in1=eq_t[:, h * 128:(h + 1) * 128], op0=ALU.mult, op1=ALU.mult),
                   reads=[qp.buf, eqb], writes=[qinb[h]])
                kp = full_rot.next()
                for kc in range(KC):
                    op("pe", lambda: pe.matmul(kp.ap(0, 128), W[:, kc * GI + 512 + h * 128:kc * GI + 512 + (h + 1) * 128],
                                               hn[:, kc, hts], start=(kc == 0), stop=(kc == KC - 1)),
                       reads=[wb, hnb[j]], writes=[kp.buf], inc=(kc == KC - 1))
                op("dve", lambda: dve.tensor_tensor(kin_t[:, h, :], kp.ap(0, 128), ek_t[:, h * 128:(h + 1) * 128], op=ALU.mult),
                   reads=[kp.buf, ekb], writes=[kinb[h]])
            ktp = full_rot.next()
            for kc in range(KC):
                op("pe", lambda: pe.matmul(ktp.ap(), hn[:, kc, hts], W[:, kc * GI + 512:kc * GI + 1024],
                                           start=(kc == 0), stop=(kc == KC - 1)),
                   reads=[wb, hnb[j]], writes=[ktp.buf], inc=(kc == KC - 1))
            op("dve", lambda: dve.tensor_tensor(kdec_t[:], ktp.ap(), gt_t[:], op=ALU.mult), reads=[ktp.buf, gtb], writes=[kdecb])
            if last and tt == 1:
                prefetch_after_current()
            def head_chain(h):
                ap_ = full_rot.next()
                op("pe", lambda: pe.matmul(ap_.ap(0, 128), kin_t[:, h, :], qin_t[:, h, :], start=True, stop=True),
                   reads=[kinb[h], qinb[h]], writes=[ap_.buf])
                yield
                att, attb = att_rot.next()
                op("dve", lambda: dve.tensor_tensor(att[:], ap_.ap(0, 128), C("maskt"), op=ALU.mult),
                   reads=[ap_.buf, cstb], writes=[attb])
                yield
                ot = full_rot.next()
                for vc in range(2):
                    op("pe", lambda: pe.matmul(ot.ap(vc * 128, (vc + 1) * 128), vsb_t[:, h * 256 + vc * 128:h * 256 + (vc + 1) * 128],
                                               att[:], start=True, stop=False, skip_group_check=True),
                       reads=[vsbb, attb], writes=[ot.buf], inc=False)
                    op("pe", lambda: pe.matmul(ot.ap(vc * 128, (vc + 1) * 128), Sbf_t[:, h, vc * 128:(vc + 1) * 128],
                                               qin_t[:, h, :], start=False, stop=True, skip_group_check=True),
                       reads=[Sbfb[h], qinb[h]], writes=[ot.buf], inc=(vc == 1))
                kvp = full_rot.next()
                op("pe", lambda: pe.matmul(kvp.ap(0, 256), kdec_t[:, h * 128:(h + 1) * 128], vsb_t[:, h * 256:(h + 1) * 256],
                                           start=True, stop=True), reads=[kdecb, vsbb], writes=[kvp.buf])
                yield
                op("dve", lambda: dve.scalar_tensor_tensor(out=S_t[:, h, :], in0=S_t[:, h, :],
                                                            scalar=eq_t[:, h * 128 + 127:h * 128 + 128], in1=kvp.ap(0, 256),
                                                            op0=ALU.mult, op1=ALU.add),
                   reads=[Sb[h], eqb, kvp.buf], writes=[Sb[h]])
                op("act", lambda: act.copy(Sbf_t[:, h, :], S_t[:, h, :]), reads=[Sb[h]], writes=[Sbfb[h]])
                yield
                osq, osqb = osq_rot.next()
                op("act", lambda: act.activation(osq[:], ot.ap(0, 256), AF.Square), reads=[ot.buf], writes=[osqb])
                osb, osbb = sx_rot.next()
                op("dve", lambda: dve.tensor_copy(osb[:], ot.ap(0, 256)), reads=[ot.buf], writes=[osbb])
                yield
                msp = full_rot.next()
                for vc in range(2):
                    op("pe", lambda: pe.matmul(msp.ap(0, 128), ones_bf[:], osq[:, vc * 128:(vc + 1) * 128],
                                               start=(vc == 0), stop=(vc == 1)),
                       reads=[onesb, osqb], writes=[msp.buf], inc=(vc == 1))
                yield
                rsh, rshb = rsh_rot.next()
                op("act", lambda: act.activation(rsh[:], msp.ap(0, 128), AF.Ln, bias=C("eps"), scale=1.0 / 256.0),
                   reads=[msp.buf, cstb], writes=[rshb])
                yield
                op("act", lambda: act.activation(rsh[:], rsh[:], AF.Exp, scale=-0.5), reads=[rshb], writes=[rshb])
                yield
                for vc in range(2):
                    c = 2 * h + vc
                    yt_full, ytb = ct_rot.next()
                    yt = yt_full[:, 0:128]
                    op("dve", lambda: dve.scalar_tensor_tensor(out=yt, in0=osb[:, vc * 128:(vc + 1) * 128],
                                                                scalar=C("gnorm", vc, vc + 1), in1=rsh[:],
                                                                op0=ALU.mult, op1=ALU.mult),
                       reads=[osbb, rshb, cstb], writes=[ytb])
                    op("dve", lambda: dve.tensor_tensor(yab[:, c, ts], yt, sr_t[:, c, ts], op=ALU.mult),
                       reads=[ytb, srb[c]], writes=[yabb[c]])
            for pair in ((0, 1), (2, 3)):
                gens = [head_chain(h) for h in pair]
                while gens:
                    for gch in list(gens):
                        try:
                            next(gch)
                        except StopIteration:
                            gens.remove(gch)
        for m in range(KC):
            pt = half_rot.next()
            for kc in range(KC):
                op("pe", lambda: pe.matmul(pt.ap(), Wo[:, kc * 1024 + m * 128:kc * 1024 + (m + 1) * 128], yab[:, kc, :],
                                           start=(kc == 0), stop=(kc == KC - 1)),
                   reads=[wob, yabb[kc]], writes=[pt.buf], inc=(kc == KC - 1))
            resid_add(j, m, pt)

    def ffn_piece(l, pi, s, on_subtile_done=None):
        n = FFN_SPLIT[pi]
        cg0 = sum(FFN_SPLIT[:pi])
        W = slots[s]
        wb = slotb[s]
        items = [(j, i) for j in range(NSUB) for i in range(n)]
        gq = {}

        def UV(idx):
            j, i = items[idx]
            up = U[idx % 2]
            vp = V[idx % 2]
            rb = [wb, hnb[j], (hnhalo if j == 0 else hnb[j - 1])]
            for kc in range(KC):
                op("pe", lambda: pe.matmul(up.ap(0, SUB + 2), W[:, kc * n * 128 + i * 128:kc * n * 128 + (i + 1) * 128],
                                           hn[:, kc, j * SUB:j * SUB + SUB + 2], start=(kc == 0), stop=(kc == KC - 1)),
                   reads=rb, writes=[up.buf], inc=(kc == KC - 1))
            vo = KC * n * 128
            for kc in range(KC):
                op("pe", lambda: pe.matmul(vp.ap(), W[:, vo + kc * n * 128 + i * 128:vo + kc * n * 128 + (i + 1) * 128],
                                           hn[:, kc, 2 + j * SUB:2 + (j + 1) * SUB], start=(kc == 0), stop=(kc == KC - 1)),
                   reads=[wb, hnb[j]], writes=[vp.buf], inc=(kc == KC - 1))
            cg = cg0 + i
            wcol = lambda tap: C("fcw", (l * 3 + tap) * NCH + cg, (l * 3 + tap) * NCH + cg + 1)
            t1, t1b = ft_rot.next()
            op("act", lambda: act.activation(t1[:], up.ap(0, SUB), AF.Identity, scale=wcol(0)),
               reads=[up.buf, cstb], writes=[t1b])
            t2, t2b = ft_rot.next()
            op("dve", lambda: dve.scalar_tensor_tensor(out=t2[:], in0=up.ap(1, SUB + 1), scalar=wcol(1), in1=t1[:],
                                                        op0=ALU.mult, op1=ALU.add),
               reads=[up.buf, t1b, cstb], writes=[t2b])
            t3, t3b = ft_rot.next()
            op("dve", lambda: dve.scalar_tensor_tensor(out=t3[:], in0=up.ap(2, SUB + 2), scalar=wcol(2), in1=t2[:],
                                                        op0=ALU.mult, op1=ALU.add),
               reads=[up.buf, t2b, cstb], writes=[t3b])
            ge, geb = ge_rot.next()
            op("act", lambda: act.activation(ge[:], t3[:], AF.Gelu, bias=C("fcb", l * NCH + cg, l * NCH + cg + 1)),
               reads=[t3b, cstb], writes=[geb])
            g, gb = g_rot.next()
            op("dve", lambda: dve.tensor_tensor(g[:], ge[:], vp.ap(), op=ALU.mult), reads=[geb, vp.buf], writes=[gb])
            gq[idx] = (g, gb)

        def Dn(idx):
            j, i = items[idx]
            g, gb = gq.pop(idx)
            do = 2 * KC * n * 128
            for m in range(KC):
                op("pe", lambda: pe.matmul(H[m].ap(), W[:, do + i * 1024 + m * 128:do + i * 1024 + (m + 1) * 128], g[:],
                                           start=(i == 0 and m % 2 == 0), stop=(i == n - 1), skip_group_check=True),
                   reads=[wb, gb], writes=[H[m].buf], inc=(m == KC - 1))
            if i == n - 1:
                for m in range(KC):
                    resid_add(j, m, H[m])
                if on_subtile_done is not None:
                    on_subtile_done(j)

        LAG = 2
        for idx in range(len(items)):
            UV(idx)
            if idx >= LAG:
                Dn(idx - LAG)
        for idx in range(max(0, len(items) - LAG), len(items)):
            Dn(idx)

    wait_x = []
    for g in range(NG):
        t0 = g * G
        for j in range(NSUB):
            tk.dma("sp", "xin%d" % j, xres[:, :, tok(j)],
                   xT[:, t0 + j * SUB:t0 + (j + 1) * SUB].rearrange("(c p) t -> p c t", p=128),
                   writes=[xb[c][j] for c in range(KC)])
        out_done = set()

        def emit_output(j, t0=t0):
            if do_final:
                rstd, rstdb = norm_stats(j)
                for c in range(KC):
                    op("dve", lambda: dve.scalar_tensor_tensor(out=xres[:, c, tok(j)], in0=xres[:, c, tok(j)],
                                                                scalar=C("finn", c, c + 1), in1=rstd[:],
                                                                op0=ALU.mult, op1=ALU.mult),
                       reads=[xb[c][j], rstdb, cstb], writes=[xb[c][j]])
            tk.dma("sp", "out%d" % j,
                   yT[:, t0 + j * SUB:t0 + (j + 1) * SUB].rearrange("(c p) t -> p c t", p=128),
                   xres[:, :, tok(j)], reads=[xb[c][j] for c in range(KC)])
            out_done.add(j)

        for l in layers:
            if debug_stage == "load":
                break
            if debug_stage == "norm":
                for j in range(NSUB):
                    norm_to_hn(j, "mixn", l)
                break
            if debug_stage == "wload":
                s = begin_piece("w_m0")
                pt = half_rot.next()
                op("pe", lambda: pe.matmul(pt.ap(), slots[s][:, 0:128], slots[s][:, 128:384], start=True, stop=True),
                   reads=[slotb[s]], writes=[pt.buf])
                op("dve", lambda: dve.tensor_copy(sx_t[0][:], pt.ap()), reads=[pt.buf], writes=[sx_rot.items[0][1]])
                break
            if debug_stage in ("mix_proj", "mix_pool", "mix_sconv"):
                s = begin_piece("w_m0")
                norm_to_hn(0, "mixn", l)
                mixer0(0, s, first_tile=True, part=debug_stage[4:])
                break
            if l % 2 == 0:
                s = begin_piece("w_m0")
                norm_to_hn(0, "mixn", l)
                for j in range(NSUB):
                    if j + 1 < NSUB:
                        norm_to_hn(j + 1, "mixn", l)
                    mixer0(j, s, first_tile=(g == 0 and j == 0))
                    if debug_stage is None:
                        norm_to_hn(j, "ffnn", l)
                if debug_stage == "mixer":
                    break
            else:
                s_in = begin_piece("w_gin")
                s_out = begin_piece("w_gout", prefetch=False)
                norm_to_hn(0, "mixn", l)
                for j in range(NSUB):
                    if j + 1 < NSUB:
                        norm_to_hn(j + 1, "mixn", l)
                    mixer1(j, s_in, s_out, last=(j == NSUB - 1))
                if debug_stage == "mix1":
                    break
            op("act", lambda: act.copy(hn[:, :, 0:2], halo_save[:, l]), reads=[halob[l]], writes=[hnhalo])
            if debug_stage is not None or l % 2 == 1:
                for j in range(NSUB):
                    norm_to_hn(j, "ffnn", l)
            op("act", lambda: act.copy(halo_save[:, l], hn[:, :, G:G + 2]), reads=[hnb[NSUB - 1]], writes=[halob[l]])
            for pi in range(len(FFN_SPLIT)):
                s = begin_piece("w_f%d_%d" % (l, pi))
                is_tail = (l == layers[-1] and pi == len(FFN_SPLIT) - 1 and debug_stage is None)
                ffn_piece(l, pi, s, on_subtile_done=(emit_output if is_tail else None))
        for j in range(NSUB):
            if j in out_done:
                continue
            if do_final and debug_stage is None:
                rstd, rstdb = norm_stats(j)
                for c in range(KC):
                    op("dve", lambda: dve.scalar_tensor_tensor(out=xres[:, c, tok(j)], in0=xres[:, c, tok(j)],
                                                                scalar=C("finn", c, c + 1), in1=rstd[:],
                                                                op0=ALU.mult, op1=ALU.mult),
                       reads=[xb[c][j], rstdb, cstb], writes=[xb[c][j]])
            tk.dma("sp", "out%d" % j,
                   yT[:, t0 + j * SUB:t0 + (j + 1) * SUB].rearrange("(c p) t -> p c t", p=128),
                   xres[:, :, tok(j)], reads=[xb[c][j] for c in range(KC)])
    tk.wait_all("sp", [xb[c][j] for c in range(KC) for j in range(NSUB)])
    es.close()
    return nc


def _run(nc, xTs, wts, names):
    in_maps = []
    for c in range(N_CORES):
        m = {"xT": xTs[c], "cst": wts["cst"], "wg2": wts["wg2"]}
        for nme in names:
            m[nme] = wts[nme]
        in_maps.append(m)
    res = run_bass_kernel_spmd(nc, in_maps, core_ids=list(range(N_CORES)))
    return [r["yT"] for r in res.results]


def _piece_names(layers):
    names = []
    for l in layers:
        names += ["w_m0"] if l % 2 == 0 else ["w_gin", "w_gout"]
        names += ["w_f%d_%d" % (l, i) for i in range(len(FFN_SPLIT))]
    return names


def kernel(**inputs):
    x = np.asarray(inputs["x"], np.float32)
    wts = prepare_host(inputs)
    xTs = [np.ascontiguousarray(x[b].T) for b in range(N_CORES)]
    nc = build_program(layers=(0, 1), do_final=True)
    ys = _run(nc, xTs, wts, _piece_names((0, 1)))
    return np.stack([np.ascontiguousarray(y.T) for y in ys], axis=0).astype(np.float32)
```

```python
import numpy as np
from contextlib import ExitStack
import concourse.bass as bass
import concourse.mybir as mybir
from concourse.bass_utils import run_bass_kernel_spmd

F32 = mybir.dt.float32
BF16 = mybir.dt.bfloat16
AF = mybir.ActivationFunctionType
ALU = mybir.AluOpType

D = 1024
S = 4096
KC = 8
DFF = 2816
NCH = 22
SUB = 256
EPS = 1e-6
N_CORES = 8
FFN_SPLIT = (8, 7, 7)
SLOT_ELEMS = 25088
SEM_EPOCH = 12000


class Buf:
    __slots__ = ("name", "w", "r")

    def __init__(self, name):
        self.name = name
        self.w = None
        self.r = {}


class TK:
    def __init__(self, nc, es):
        self.nc = nc
        self.es = es
        self.eng = {"pe": nc.tensor, "act": nc.scalar, "dve": nc.vector, "pool": nc.gpsimd, "sp": nc.sync}
        self.sems = {}
        self.cnt = {}
        self.epoch = {k: 0 for k in self.eng}
        self.seen = {k: {} for k in self.eng}
        for k in self.eng:
            self._new_sem((k, 0))

    def _new_sem(self, key):
        self.sems[key] = self.es.enter_context(self.nc.semaphore("s_%s_%s" % key))
        self.cnt[key] = 0

    def dma_key(self, name):
        key = (name, 0)
        if key not in self.sems:
            self._new_sem(key)
        return key

    def _deps(self, e, reads, writes):
        deps = {}

        def add(st):
            if st is None:
                return
            k, v = st
            if deps.get(k, 0) < v:
                deps[k] = v

        for b in reads:
            add(b.w)
        for b in writes:
            add(b.w)
            for k, v in b.r.items():
                if k[0] == e:
                    continue
                add((k, v))
        return deps

    def _wait(self, e, deps):
        for k, v in deps.items():
            if k[0] == e and e == "pe":
                continue
            if self.seen[e].get(k, 0) >= v:
                continue
            self.eng[e].wait_ge(self.sems[k], v)
            self.seen[e][k] = v

    def op(self, e, fn, reads=(), writes=(), inc=True):
        self._wait(e, self._deps(e, reads, writes))
        ins = fn()
        key = (e, self.epoch[e])
        if inc:
            ins.then_inc(self.sems[key], 1)
            self.cnt[key] += 1
            stamp = (key, self.cnt[key])
            if self.cnt[key] >= SEM_EPOCH:
                self.epoch[e] += 1
                self._new_sem((e, self.epoch[e]))
        else:
            stamp = (key, self.cnt[key] + 1)
        for b in reads:
            if b.r.get(stamp[0], 0) < stamp[1]:
                b.r[stamp[0]] = stamp[1]
        for b in writes:
            b.w = stamp
            b.r = {}
        return ins

    def dma(self, q, name, out_ap, in_ap, reads=(), writes=()):
        key = self.dma_key(name)
        self._wait(q, self._deps(q, reads, writes))
        self.eng[q].dma_start(out=out_ap, in_=in_ap).then_inc(self.sems[key], 16)
        self.cnt[key] += 16
        stamp = (key, self.cnt[key])
        for b in reads:
            if b.r.get(key, 0) < stamp[1]:
                b.r[key] = stamp[1]
        for b in writes:
            b.w = stamp
            b.r = {}

    def wait_all(self, e, bufs):
        deps = {}
        for b in bufs:
            sts = [b.w] + list(b.r.items())
            for st in sts:
                if st is None:
                    continue
                k, v = st
                if deps.get(k, 0) < v:
                    deps[k] = v
        for k, v in deps.items():
            self.eng[e].wait_ge(self.sems[k], v)


class Rot:
    def __init__(self, items):
        self.items = items
        self.i = 0

    def next(self):
        it = self.items[self.i % len(self.items)]
        self.i += 1
        return it


def _cst_layout(depth):
    off = {}
    o = 0

    def add(name, n):
        nonlocal o
        off[name] = (o, n)
        o += n

    add("mixn", depth * 8)
    add("ffnn", depth * 8)
    add("finn", 8)
    add("pool_b", 4)
    add("pool_s", 4)
    add("sc_w", 12)
    add("sc_b", 4)
    add("fcw", depth * 3 * NCH)
    add("fcb", depth * NCH)
    add("gnorm", 2)
    add("eps", 1)
    add("invd", 64)
    add("bg", 512)
    add("trin", 128)
    add("revn", 128)
    add("maskt", 128)
    return off, o


def _pm(v, nchunk):
    return np.ascontiguousarray(np.asarray(v, np.float32).reshape(nchunk, 128).T)


def _wk(w):
    K, F = w.shape
    return np.ascontiguousarray(w.reshape(K // 128, 128, F).transpose(1, 0, 2).reshape(128, -1))


def prepare_host(inp):
    depth = 2
    off, ncst = _cst_layout(depth)
    cst = np.zeros((128, ncst), np.float32)

    def put(name, arr):
        o, n = off[name]
        assert arr.shape == (128, n), (name, arr.shape, n)
        cst[:, o:o + n] = arr

    put("mixn", np.concatenate([_pm(inp["mix_norm"][l], 8) for l in range(depth)], axis=1))
    put("ffnn", np.concatenate([_pm(inp["ffn_norm"][l], 8) for l in range(depth)], axis=1))
    put("finn", _pm(inp["final_norm"], 8))
    put("pool_b", _pm(inp["pool_b"][0], 4))
    put("pool_s", _pm(inp["pool_scale"][0], 4))
    put("sc_w", np.concatenate([_pm(inp["sc_conv_w"][0][i], 4) for i in range(3)], axis=1))
    put("sc_b", _pm(inp["sc_conv_b"][0], 4))
    put("fcw", np.concatenate([_pm(inp["ffn_conv_w"][l][i], NCH) for l in range(depth) for i in range(3)], axis=1))
    put("fcb", np.concatenate([_pm(inp["ffn_conv_b"][l], NCH) for l in range(depth)], axis=1))
    put("gnorm", _pm(inp["gla_norm"][0], 2))
    put("eps", np.full((128, 1), EPS, np.float32))
    invd = np.zeros((128, 64), np.float32)
    for gi, win in enumerate((2, 4, 8, 16)):
        t = np.arange(1, 17, dtype=np.float32)
        invd[:, gi * 16:(gi + 1) * 16] = 1.0 / np.minimum(t, float(win))
    put("invd", invd)
    put("bg", np.ascontiguousarray(np.broadcast_to(np.asarray(inp["gla_b_g"][0], np.float32)[None, :], (128, 512))))
    s_i = np.arange(128)[:, None]
    t_i = np.arange(128)[None, :]
    put("trin", np.where(s_i <= t_i, -1.0 / 16.0, 0.0).astype(np.float32))
    put("revn", np.where(s_i > t_i, -1.0 / 16.0, 0.0).astype(np.float32))
    put("maskt", np.where(s_i <= t_i, 1.0, 0.0).astype(np.float32))

    wts = {}
    wts["w_m0"] = np.concatenate([
        _wk(np.asarray(inp["ab_w_in"][0], np.float32)),
        np.ascontiguousarray(np.asarray(inp["pool_w"][0], np.float32).transpose(1, 0, 2).reshape(128, 512)),
        _wk(np.asarray(inp["ab_w_out"][0], np.float32)),
    ], axis=1)
    for l in range(depth):
        wu = np.asarray(inp["ffn_w_up"][l], np.float32)
        wd = np.asarray(inp["ffn_w_down"][l], np.float32)
        c0 = 0
        for i, n in enumerate(FFN_SPLIT):
            u = _wk(wu[:, c0 * 128:(c0 + n) * 128])
            v = _wk(wu[:, DFF + c0 * 128:DFF + (c0 + n) * 128])
            dn = _wk(wd[c0 * 128:(c0 + n) * 128, :])
            wts["w_f%d_%d" % (l, i)] = np.concatenate([u, v, dn], axis=1)
            c0 += n
    wts["w_gin"] = _wk(np.asarray(inp["gla_w_in"][0], np.float32))
    wts["w_gout"] = _wk(np.asarray(inp["gla_w_out"][0], np.float32))
    wts["wg2"] = np.ascontiguousarray(np.asarray(inp["gla_w_g2"][0], np.float32))
    wts["cst"] = cst
    return wts


def build_program(layers=(0, 1), do_final=True, G=1024, n_groups=None, debug_stage=None):
    depth = 2
    NSUB = G // SUB
    NG = S // G if n_groups is None else n_groups
    coff, ncst = _cst_layout(depth)
    nc = bass.Bass("TRN2", target_bir_lowering=False)
    es = ExitStack()
    tk = TK(nc, es)

    def dram(name, shape, kind):
        return nc.dram_tensor(name, list(shape), F32, kind=kind).ap()

    xT = dram("xT", [D, S], "ExternalInput")
    yT = dram("yT", [D, S], "ExternalOutput")
    cst_d = dram("cst", [128, ncst], "ExternalInput")
    wg2_d = dram("wg2", [16, 512], "ExternalInput")
    piece_cols = {"w_m0": 25088, "w_gin": 24704, "w_gout": 8192}
    for l in range(depth):
        for i, n in enumerate(FFN_SPLIT):
            piece_cols["w_f%d_%d" % (l, i)] = 3072 * n
    piece_d = {}
    need = set()
    for l in layers:
        if l % 2 == 0:
            need.add("w_m0")
        else:
            need.update(["w_gin", "w_gout"])
        for i in range(len(FFN_SPLIT)):
            need.add("w_f%d_%d" % (l, i))
    for name in sorted(need):
        piece_d[name] = dram(name, [128, piece_cols[name]], "ExternalInput")

    def sb(name, shape, dt):
        return es.enter_context(nc.sbuf_tensor(name, list(shape), dt))

    xres = sb("xres", [128, KC, G], F32)
    hn = sb("hn", [128, KC, G + 2], BF16)
    slots = [sb("slotA", [128, SLOT_ELEMS], BF16), sb("slotB", [128, SLOT_ELEMS], BF16)]
    cst = sb("cst_sb", [128, ncst], F32)
    ones_bf = sb("ones_bf", [128, 128], BF16)
    halo_save = sb("halo_save", [128, depth, KC, 2], BF16)
    sq_t = [sb("sq%d" % i, [128, SUB], BF16) for i in range(2)]
    ln_t = sb("ln_t", [128, SUB], F32)
    rstd_t = [sb("rstd%d" % i, [128, SUB], F32) for i in range(2)]
    ft_t = [sb("ft%d" % i, [128, SUB], F32) for i in range(3)]
    ge_t = [sb("ge%d" % i, [128, SUB], F32) for i in range(2)]
    g_t = [sb("g%d" % i, [128, SUB], BF16) for i in range(3)]
    pbuf = sb("pbuf", [128, 4, 272], F32)
    zbuf = sb("zbuf", [128, 4, 258], F32)
    pt_t = [sb("pt%d" % i, [128, 272], F32) for i in range(2)]
    pfix_t = sb("pfix", [128, 16], F32)
    pbf_t = [sb("pbf%d" % i, [128, SUB], BF16) for i in range(2)]
    yab = sb("yab", [128, KC, SUB], BF16)
    sx_t = [sb("sx%d" % i, [128, SUB], F32) for i in range(2)]
    ct_t = [sb("ct%d" % i, [128, SUB], F32) for i in range(2)]

    has_gla = any(l % 2 == 1 for l in layers)
    if has_gla:
        wg2_sb = sb("wg2_sb", [16, 512], BF16)
        gl_sb = sb("gl_sb", [16, SUB], BF16)
        gt_t = sb("gt_t", [128, 512], F32)
        eq_t = sb("eq_t", [128, 512], F32)
        ek_t = sb("ek_t", [128, 512], F32)
        qin_t = sb("qin_t", [128, 4, 128], BF16)
        kin_t = sb("kin_t", [128, 4, 128], BF16)
        kdec_t = sb("kdec_t", [128, 512], BF16)
        vsb_t = sb("vsb_t", [128, 1024], BF16)
        att_t = [sb("att%d" % i, [128, 128], BF16) for i in range(2)]
        S_t = sb("S_t", [128, 4, 256], F32)
        Sbf_t = sb("Sbf_t", [128, 4, 256], BF16)
        osq_t = [sb("osq%d" % i, [128, 256], BF16) for i in range(2)]
        rsh_t = [sb("rsh%d" % i, [128, 128], F32) for i in range(2)]
        sr_t = sb("sr_t", [128, KC, SUB], BF16)

    banks = [es.enter_context(nc.psum_tensor("bank%d" % i, [128, 512], F32)) for i in range(8)]

    bankbuf = [Buf("bank%d" % i) for i in range(8)]

    class PT:
        def __init__(self, bi, c0, n):
            self.bank, self.c0, self.n = banks[bi], c0, n
            self.buf = bankbuf[bi]

        def ap(self, a=0, b=None, p0=0, p1=128):
            b = self.n if b is None else b
            return self.bank[p0:p1, self.c0 + a:self.c0 + b]

    H = [PT(i // 2, (i % 2) * 256, 256) for i in range(8)]
    U = [PT(4, 0, 512), PT(5, 0, 512)]
    V = [PT(6, 0, 256), PT(7, 0, 256)]
    half_rot = Rot([PT(b, 0, 256) for b in range(8)])
    m_rot = Rot([PT(6, 256, 256), PT(7, 256, 256)])
    full_rot = Rot([PT(b, 0, 512) for b in range(8)])

    xb = [[Buf("x%d_%d" % (c, j)) for j in range(NSUB)] for c in range(KC)]
    hnb = [Buf("hn%d" % j) for j in range(NSUB)]
    hnhalo = Buf("hnhalo")
    slotb = [Buf("slotA"), Buf("slotB")]
    cstb = Buf("cst")
    onesb = Buf("ones")
    halob = [Buf("halo_save%d" % l) for l in range(depth)]
    sq_rot = Rot([(t, Buf("sq")) for t in sq_t])
    lnb = Buf("ln")
    rstd_rot = Rot([(t, Buf("rstd")) for t in rstd_t])
    ft_rot = Rot([(t, Buf("ft")) for t in ft_t])
    ge_rot = Rot([(t, Buf("ge")) for t in ge_t])
    g_rot = Rot([(t, Buf("g")) for t in g_t])
    pbufb = [Buf("pbuf%d" % i) for i in range(4)]
    zbufb = [Buf("zbuf%d" % i) for i in range(4)]
    pt_rot = Rot([(t, Buf("pt")) for t in pt_t])
    pfixb = Buf("pfix")
    pbf_rot = Rot([(t, Buf("pbf")) for t in pbf_t])
    yabb = [Buf("yab%d" % i) for i in range(KC)]
    sx_rot = Rot([(t, Buf("sx")) for t in sx_t])
    ct_rot = Rot([(t, Buf("ct")) for t in ct_t])

    if has_gla:
        wg2b, glb, gtb, eqb, ekb = (Buf(n) for n in ("wg2", "gl", "gt", "eq", "ek"))
        qinb = [Buf("qin%d" % h) for h in range(4)]
        kinb = [Buf("kin%d" % h) for h in range(4)]
        kdecb, vsbb = Buf("kdec"), Buf("vsb")
        att_rot = Rot([(t, Buf("att")) for t in att_t])
        Sb = [Buf("S%d" % h) for h in range(4)]
        Sbfb = [Buf("Sbf%d" % h) for h in range(4)]
        osq_rot = Rot([(t, Buf("osq")) for t in osq_t])
        rsh_rot = Rot([(t, Buf("rsh")) for t in rsh_t])
        srb = [Buf("sr%d" % c) for c in range(KC)]

    def C(name, a=0, b=None):
        o, n = coff[name]
        b = n if b is None else b
        return cst[:, o + a:o + b]

    op = tk.op
    act, dve, pe = nc.scalar, nc.vector, nc.tensor

    tk.dma("sp", "cst", cst[:], cst_d, writes=[cstb])
    op("dve", lambda: dve.memset(ones_bf[:], 1.0), writes=[onesb])
    op("dve", lambda: dve.memset(halo_save[:], 0.0), writes=halob)
    op("dve", lambda: dve.memset(pbuf[:], 0.0), writes=pbufb)
    op("dve", lambda: dve.memset(zbuf[:], 0.0), writes=zbufb)

    if has_gla:
        tk.dma("pool", "wg2", wg2_sb[:], wg2_d, writes=[wg2b])
        op("dve", lambda: dve.memset(S_t[:], 0.0), writes=Sb)
        op("dve", lambda: dve.memset(Sbf_t[:], 0.0), writes=Sbfb)

    piece_seq = []
    for g in range(NG):
        for l in layers:
            if l % 2 == 0:
                piece_seq.append("w_m0")
            else:
                piece_seq += ["w_gin", "w_gout"]
            for i in range(len(FFN_SPLIT)):
                piece_seq.append("w_f%d_%d" % (l, i))
    wstate = {"next": 0}

    def issue_piece():
        i = wstate["next"]
        if i >= len(piece_seq):
            return
        wstate["next"] = i + 1
        name = piece_seq[i]
        s = i % 2
        ncols = piece_cols[name]
        c0 = 0
        while c0 < ncols:
            c1 = min(ncols, c0 + 4096)
            tk.dma("pool", "w%d" % s, slots[s][:, c0:c1], piece_d[name][:, c0:c1], writes=[slotb[s]])
            c0 = c1

    cur_piece = {"i": -1}

    def begin_piece(name, prefetch=True):
        cur_piece["i"] += 1
        i = cur_piece["i"]
        assert piece_seq[i] == name, (piece_seq[i], name)
        lim = i + 1 if prefetch else i
        while wstate["next"] <= lim and wstate["next"] < len(piece_seq):
            issue_piece()
        return i % 2

    def prefetch_after_current():
        while wstate["next"] <= cur_piece["i"] + 1 and wstate["next"] < len(piece_seq):
            issue_piece()

    def tok(j):
        return slice(j * SUB, (j + 1) * SUB)

    def norm_stats(j, act_only=False):
        ms = m_rot.next()
        for c in range(KC):
            sq, sqb = sq_rot.next()
            if act_only or c % 2 == 0:
                op("act", lambda: act.activation(sq[:], xres[:, c, tok(j)], AF.Square),
                   reads=[xb[c][j]], writes=[sqb])
            else:
                op("dve", lambda: dve.tensor_tensor(sq[:], xres[:, c, tok(j)], xres[:, c, tok(j)], op=ALU.mult),
                   reads=[xb[c][j]], writes=[sqb])
            op("pe", lambda: pe.matmul(ms.ap(), ones_bf[:], sq[:], start=(c == 0), stop=(c == KC - 1)),
               reads=[onesb, sqb], writes=[ms.buf], inc=True)
        op("act", lambda: act.activation(ln_t[:], ms.ap(), AF.Ln, bias=C("eps"), scale=1.0 / D),
           reads=[ms.buf, cstb], writes=[lnb])
        rstd, rstdb = rstd_rot.next()
        op("act", lambda: act.activation(rstd[:], ln_t[:], AF.Exp, scale=-0.5), reads=[lnb], writes=[rstdb])
        return rstd, rstdb

    def norm_to_hn(j, gname, l):
        rstd, rstdb = norm_stats(j, act_only=(l % 2 == 0))
        for c in range(KC):
            gcol = C(gname, l * 8 + c, l * 8 + c + 1)
            op("dve", lambda: dve.scalar_tensor_tensor(out=hn[:, c, 2 + j * SUB:2 + (j + 1) * SUB],
                                                        in0=xres[:, c, tok(j)], scalar=gcol, in1=rstd[:],
                                                        op0=ALU.mult, op1=ALU.mult),
               reads=[xb[c][j], rstdb, cstb], writes=[hnb[j]])

    def resid_add(j, m, pt):
        op("dve", lambda: dve.tensor_tensor(xres[:, m, tok(j)], xres[:, m, tok(j)], pt.ap(), op=ALU.add),
           reads=[xb[m][j], pt.buf], writes=[xb[m][j]])

    def mixer0(j, s, first_tile, part=None):
        W = slots[s]
        wb = slotb[s]
        hsl = slice(2 + j * SUB, 2 + (j + 1) * SUB)

        def proj(col0):
            pt = half_rot.next()
            for kc in range(KC):
                op("pe", lambda: pe.matmul(pt.ap(), W[:, kc * 2048 + col0:kc * 2048 + col0 + 128], hn[:, kc, hsl],
                                           start=(kc == 0), stop=(kc == KC - 1)),
                   reads=[wb, hnb[j]], writes=[pt.buf], inc=(kc == KC - 1))
            return pt

        if part == "proj":
            proj(0)
            return
        def pool_chain(gi):
            win = 2 << gi
            L = gi + 1
            pu = proj(gi * 128)
            yield
            op("act", lambda: act.copy(pbuf[:, gi, 16:272], pu.ap()), reads=[pu.buf], writes=[pbufb[gi]])
            yield
            hist = [0] * (L + 1)
            for k in range(L, 0, -1):
                hist[k - 1] = hist[k] + (1 << (k - 1))
            prev = (lambda a, b, gi=gi, h0=hist[0]: pbuf[:, gi, 16 - h0 + a:16 - h0 + b])
            prevb = pbufb[gi]
            for k in range(1, L + 1):
                sh = 1 << (k - 1)
                n = SUB + hist[k]
                cur, curb = pt_rot.next()
                op("dve", lambda: dve.tensor_tensor(cur[:, 0:n], prev(sh, sh + n), prev(0, n), op=ALU.add),
                   reads=[prevb], writes=[curb])
                prev = (lambda a, b, cur=cur: cur[:, a:b])
                prevb = curb
                yield
            pbf, pbfb = pbf_rot.next()
            op("dve", lambda: dve.scalar_tensor_tensor(out=pbf[:], in0=prev(0, SUB), scalar=1.0 / win,
                                                        in1=pbuf[:, gi, 16:272], op0=ALU.mult, op1=ALU.subtract),
               reads=[prevb, pbufb[gi]], writes=[pbfb])
            yield
            if first_tile:
                op("dve", lambda: dve.tensor_tensor(pfix_t[:], prev(0, 16), C("invd", gi * 16, gi * 16 + 16), op=ALU.mult),
                   reads=[prevb, cstb], writes=[pfixb])
                op("dve", lambda: dve.tensor_tensor(pbf[:, 0:16], pfix_t[:], pbuf[:, gi, 16:32], op=ALU.subtract),
                   reads=[pfixb, pbufb[gi], pbfb], writes=[pbfb])
                yield
            op("act", lambda: act.copy(pbuf[:, gi, 0:16], pbuf[:, gi, 256:272]), reads=[pbufb[gi]], writes=[pbufb[gi]])
            ya = half_rot.next()
            op("pe", lambda: pe.matmul(ya.ap(), W[:, 16384 + gi * 128:16384 + (gi + 1) * 128], pbf[:],
                                       start=True, stop=True), reads=[wb, pbfb], writes=[ya.buf])
            yield
            op("dve", lambda: dve.tensor_scalar(yab[:, gi, :], ya.ap(), C("pool_b", gi, gi + 1), C("pool_s", gi, gi + 1),
                                                op0=ALU.add, op1=ALU.mult),
               reads=[ya.buf, cstb], writes=[yabb[gi]])

        def conv_chain(ci):
            sbp = proj(512 + ci * 128)
            yield
            scp = proj(1024 + ci * 128)
            yield
            sxp = proj(1536 + ci * 128)
            yield
            sx, sxb = sx_rot.next()
            op("act", lambda: act.copy(sx[:], sxp.ap()), reads=[sxp.buf], writes=[sxb])
            yield
            op("dve", lambda: dve.tensor_tensor(zbuf[:, ci, 2:258], scp.ap(), sx[:], op=ALU.mult),
               reads=[scp.buf, sxb], writes=[zbufb[ci]])
            yield
            t1, t1b = ct_rot.next()
            op("act", lambda: act.activation(t1[:], zbuf[:, ci, 2:258], AF.Identity,
                                             bias=C("sc_b", ci, ci + 1), scale=C("sc_w", 8 + ci, 9 + ci)),
               reads=[zbufb[ci], cstb], writes=[t1b])
            yield
            t2, t2b = ct_rot.next()
            op("dve", lambda: dve.scalar_tensor_tensor(out=t2[:], in0=zbuf[:, ci, 1:257], scalar=C("sc_w", 4 + ci, 5 + ci),
                                                        in1=t1[:], op0=ALU.mult, op1=ALU.add),
               reads=[zbufb[ci], t1b, cstb], writes=[t2b])
            yield
            op("dve", lambda: dve.scalar_tensor_tensor(out=t1[:], in0=zbuf[:, ci, 0:256], scalar=C("sc_w", ci, ci + 1),
                                                        in1=t2[:], op0=ALU.mult, op1=ALU.add),
               reads=[zbufb[ci], t2b, cstb], writes=[t1b])
            yield
            op("dve", lambda: dve.tensor_tensor(yab[:, 4 + ci, :], t1[:], sbp.ap(), op=ALU.mult),
               reads=[t1b, sbp.buf], writes=[yabb[4 + ci]])
            op("act", lambda: act.copy(zbuf[:, ci, 0:2], zbuf[:, ci, 256:258]), reads=[zbufb[ci]], writes=[zbufb[ci]])

        for i in range(4):
            gens = [pool_chain(i), conv_chain(i)]
            while gens:
                for gch in list(gens):
                    try:
                        next(gch)
                    except StopIteration:
                        gens.remove(gch)
        for m in range(KC):
            pt = half_rot.next()
            for kc in range(KC):
                op("pe", lambda: pe.matmul(pt.ap(), W[:, 16896 + kc * 1024 + m * 128:16896 + kc * 1024 + (m + 1) * 128],
                                           yab[:, kc, :], start=(kc == 0), stop=(kc == KC - 1)),
                   reads=[wb, yabb[kc]], writes=[pt.buf], inc=(kc == KC - 1))
            resid_add(j, m, pt)

    def mixer1(j, s_in, s_out, last=False):
        W, wb = slots[s_in], slotb[s_in]
        Wo, wob = slots[s_out], slotb[s_out]
        GI = 3088
        hsl = slice(2 + j * SUB, 2 + (j + 1) * SUB)
        QS = 128.0 ** -0.5

        def proj_fm(col0, mrows=128):
            pt = full_rot.next()
            for kc in range(KC):
                op("pe", lambda: pe.matmul(pt.ap(0, SUB, 0, mrows), W[:, kc * GI + col0:kc * GI + col0 + mrows], hn[:, kc, hsl],
                                           start=(kc == 0), stop=(kc == KC - 1)),
                   reads=[wb, hnb[j]], writes=[pt.buf], inc=(kc == KC - 1))
            return pt

        glp = proj_fm(3072, 16)
        op("act", lambda: act.copy(gl_sb[:], glp.ap(0, SUB, 0, 16)), reads=[glp.buf], writes=[glb])
        for c in range(KC):
            rp = proj_fm(2048 + c * 128)
            op("act", lambda: act.activation(sr_t[:, c, :], rp.ap(0, SUB), AF.Silu), reads=[rp.buf], writes=[srb[c]])

        for tt in range(2):
            ts = slice(tt * 128, (tt + 1) * 128)
            hts = slice(2 + j * SUB + tt * 128, 2 + j * SUB + (tt + 1) * 128)
            zp = full_rot.next()
            op("pe", lambda: pe.matmul(zp.ap(), gl_sb[:, ts], wg2_sb[:], start=True, stop=True),
               reads=[glb, wg2b], writes=[zp.buf])
            vtps = []
            for half in range(2):
                vtp = full_rot.next()
                for kc in range(KC):
                    op("pe", lambda: pe.matmul(vtp.ap(), hn[:, kc, hts],
                                               W[:, kc * GI + 1024 + half * 512:kc * GI + 1024 + (half + 1) * 512],
                                               start=(kc == 0), stop=(kc == KC - 1)),
                       reads=[wb, hnb[j]], writes=[vtp.buf], inc=(kc == KC - 1))
                vtps.append(vtp)
            op("dve", lambda: dve.tensor_tensor(gt_t[:], zp.ap(), C("bg"), op=ALU.add), reads=[zp.buf, cstb], writes=[gtb])
            op("act", lambda: act.activation(gt_t[:], gt_t[:], AF.Exp, scale=-1.0), reads=[gtb], writes=[gtb])
            op("act", lambda: act.activation(gt_t[:], gt_t[:], AF.Ln, bias=1.0), reads=[gtb], writes=[gtb])
            bc = full_rot.next()
            for h in range(4):
                op("pe", lambda: pe.matmul(bc.ap(h * 128, (h + 1) * 128), gt_t[:, h * 128:(h + 1) * 128], C("trin"),
                                           start=True, stop=True, skip_group_check=True),
                   reads=[gtb, cstb], writes=[bc.buf], inc=(h == 3))
            rv = full_rot.next()
            op("pe", lambda: pe.matmul(rv.ap(), C("revn"), gt_t[:], start=True, stop=True), reads=[gtb, cstb], writes=[rv.buf])
            op("act", lambda: act.activation(eq_t[:], bc.ap(), AF.Exp), reads=[bc.buf], writes=[eqb])
            op("act", lambda: act.activation(ek_t[:], bc.ap(), AF.Exp, scale=-1.0), reads=[bc.buf], writes=[ekb])
            op("act", lambda: act.activation(gt_t[:], rv.ap(), AF.Exp), reads=[rv.buf], writes=[gtb])
            for half in range(2):
                op("act", lambda: act.copy(vsb_t[:, half * 512:(half + 1) * 512], vtps[half].ap()),
                   reads=[vtps[half].buf], writes=[vsbb])
            for h in range(4):
                qp = full_rot.next()
                for kc in range(KC):
                    op("pe", lambda: pe.matmul(qp.ap(0, 128), W[:, kc * GI + h * 128:kc * GI + (h + 1) * 128], hn[:, kc, hts],
                                               start=(kc == 0), stop=(kc == KC - 1)),
                       reads=[wb, hnb[j]], writes=[qp.buf], inc=(kc == KC - 1))
                op("dve", lambda: dve.scalar_tensor_tensor(out=qin_t[:, h, :], in0=qp.ap(0, 128), scalar=QS,
                                                            in1=eq_t[:, h * 128:(h + 1) * 128], op0=ALU.mult, op1=ALU.mult),
                   reads=[qp.buf, eqb], writes=[qinb[h]])
                kp = full_rot.next()
                for kc in range(KC):
                    op("pe", lambda: pe.matmul(kp.ap(0, 128), W[:, kc * GI + 512 + h * 128:kc * GI + 512 + (h + 1) * 128],
                                               hn[:, kc, hts], start=(kc == 0), stop=(kc == KC - 1)),
                       reads=[wb, hnb[j]], writes=[kp.buf], inc=(kc == KC - 1))
                op("dve", lambda: dve.tensor_tensor(kin_t[:, h, :], kp.ap(0, 128), ek_t[:, h * 128:(h + 1) * 128], op=ALU.mult),
                   reads=[kp.buf, ekb], writes=[kinb[h]])
            ktp = full_rot.next()
            for kc in range(KC):
                op("pe", lambda: pe.matmul(ktp.ap(), hn[:, kc, hts], W[:, kc * GI + 512:kc * GI + 1024],
                                           start=(kc == 0), stop=(kc == KC - 1)),
                   reads=[wb, hnb[j]], writes=[ktp.buf], inc=(kc == KC - 1))
            op("dve", lambda: dve.tensor_tensor(kdec_t[:], ktp.ap(), gt_t[:], op=ALU.mult), reads=[ktp.buf, gtb], writes=[kdecb])
            if last and tt == 1:
                prefetch_after_current()
            def head_chain(h):
                ap_ = full_rot.next()
                op("pe", lambda: pe.matmul(ap_.ap(0, 128), kin_t[:, h, :], qin_t[:, h, :], start=True, stop=True),
                   reads=[kinb[h], qinb[h]], writes=[ap_.buf])
                yield
                att, attb = att_rot.next()
                op("dve", lambda: dve.tensor_tensor(att[:], ap_.ap(0, 128), C("maskt"), op=ALU.mult),
                   reads=[ap_.buf, cstb], writes=[attb])
                yield
                ot = full_rot.next()
                for vc in range(2):
                    op("pe", lambda: pe.matmul(ot.ap(vc * 128, (vc + 1) * 128), vsb_t[:, h * 256 + vc * 128:h * 256 + (vc + 1) * 128],
                                               att[:], start=True, stop=False, skip_group_check=True),
                       reads=[vsbb, attb], writes=[ot.buf], inc=False)
                    op("pe", lambda: pe.matmul(ot.ap(vc * 128, (vc + 1) * 128), Sbf_t[:, h, vc * 128:(vc + 1) * 128],
                                               qin_t[:, h, :], start=False, stop=True, skip_group_check=True),
                       reads=[Sbfb[h], qinb[h]], writes=[ot.buf], inc=(vc == 1))
                kvp = full_rot.next()
                op("pe", lambda: pe.matmul(kvp.ap(0, 256), kdec_t[:, h * 128:(h + 1) * 128], vsb_t[:, h * 256:(h + 1) * 256],
                                           start=True, stop=True), reads=[kdecb, vsbb], writes=[kvp.buf])
                yield
                op("dve", lambda: dve.scalar_tensor_tensor(out=S_t[:, h, :], in0=S_t[:, h, :],
                                                            scalar=eq_t[:, h * 128 + 127:h * 128 + 128], in1=kvp.ap(0, 256),
                                                            op0=ALU.mult, op1=ALU.add),
                   reads=[Sb[h], eqb, kvp.buf], writes=[Sb[h]])
                op("act", lambda: act.copy(Sbf_t[:, h, :], S_t[:, h, :]), reads=[Sb[h]], writes=[Sbfb[h]])
                yield
                osq, osqb = osq_rot.next()
                op("act", lambda: act.activation(osq[:], ot.ap(0, 256), AF.Square), reads=[ot.buf], writes=[osqb])
                osb, osbb = sx_rot.next()
                op("dve", lambda: dve.tensor_copy(osb[:], ot.ap(0, 256)), reads=[ot.buf], writes=[osbb])
                yield
                msp = full_rot.next()
                for vc in range(2):
                    op("pe", lambda: pe.matmul(msp.ap(0, 128), ones_bf[:], osq[:, vc * 128:(vc + 1) * 128],
                                               start=(vc == 0), stop=(vc == 1)),
                       reads=[onesb, osqb], writes=[msp.buf], inc=(vc == 1))
                yield
                rsh, rshb = rsh_rot.next()
                op("act", lambda: act.activation(rsh[:], msp.ap(0, 128), AF.Ln, bias=C("eps"), scale=1.0 / 256.0),
                   reads=[msp.buf, cstb], writes=[rshb])
                yield
                op("act", lambda: act.activation(rsh[:], rsh[:], AF.Exp, scale=-0.5), reads=[rshb], writes=[rshb])
                yield
                for vc in range(2):
                    c = 2 * h + vc
                    yt_full, ytb = ct_rot.next()
                    yt = yt_full[:, 0:128]
                    op("dve", lambda: dve.scalar_tensor_tensor(out=yt, in0=osb[:, vc * 128:(vc + 1) * 128],
                                                                scalar=C("gnorm", vc, vc + 1), in1=rsh[:],
                                                                op0=ALU.mult, op1=ALU.mult),
                       reads=[osbb, rshb, cstb], writes=[ytb])
                    op("dve", lambda: dve.tensor_tensor(yab[:, c, ts], yt, sr_t[:, c, ts], op=ALU.mult),
                       reads=[ytb, srb[c]], writes=[yabb[c]])
            for pair in ((0, 1), (2, 3)):
                gens = [head_chain(h) for h in pair]
                while gens:
                    for gch in list(gens):
                        try:
                            next(gch)
                        except StopIteration:
                            gens.remove(gch)
        for m in range(KC):
            pt = half_rot.next()
            for kc in range(KC):
                op("pe", lambda: pe.matmul(pt.ap(), Wo[:, kc * 1024 + m * 128:kc * 1024 + (m + 1) * 128], yab[:, kc, :],
                                           start=(kc == 0), stop=(kc == KC - 1)),
                   reads=[wob, yabb[kc]], writes=[pt.buf], inc=(kc == KC - 1))
            resid_add(j, m, pt)

    def ffn_piece(l, pi, s, on_subtile_done=None):
        n = FFN_SPLIT[pi]
        cg0 = sum(FFN_SPLIT[:pi])
        W = slots[s]
        wb = slotb[s]
        items = [(j, i) for j in range(NSUB) for i in range(n)]
        gq = {}

        def UV(idx):
            j, i = items[idx]
            up = U[idx % 2]
            vp = V[idx % 2]
            rb = [wb, hnb[j], (hnhalo if j == 0 else hnb[j - 1])]
            for kc in range(KC):
                op("pe", lambda: pe.matmul(up.ap(0, SUB + 2), W[:, kc * n * 128 + i * 128:kc * n * 128 + (i + 1) * 128],
                                           hn[:, kc, j * SUB:j * SUB + SUB + 2], start=(kc == 0), stop=(kc == KC - 1)),
                   reads=rb, writes=[up.buf], inc=(kc == KC - 1))
            vo = KC * n * 128
            for kc in range(KC):
                op("pe", lambda: pe.matmul(vp.ap(), W[:, vo + kc * n * 128 + i * 128:vo + kc * n * 128 + (i + 1) * 128],
                                           hn[:, kc, 2 + j * SUB:2 + (j + 1) * SUB], start=(kc == 0), stop=(kc == KC - 1)),
                   reads=[wb, hnb[j]], writes=[vp.buf], inc=(kc == KC - 1))
            cg = cg0 + i
            wcol = lambda tap: C("fcw", (l * 3 + tap) * NCH + cg, (l * 3 + tap) * NCH + cg + 1)
            t1, t1b = ft_rot.next()
            op("act", lambda: act.activation(t1[:], up.ap(0, SUB), AF.Identity, scale=wcol(0)),
               reads=[up.buf, cstb], writes=[t1b])
            t2, t2b = ft_rot.next()
            op("dve", lambda: dve.scalar_tensor_tensor(out=t2[:], in0=up.ap(1, SUB + 1), scalar=wcol(1), in1=t1[:],
                                                        op0=ALU.mult, op1=ALU.add),
               reads=[up.buf, t1b, cstb], writes=[t2b])
            t3, t3b = ft_rot.next()
            op("dve", lambda: dve.scalar_tensor_tensor(out=t3[:], in0=up.ap(2, SUB + 2), scalar=wcol(2), in1=t2[:],
                                                        op0=ALU.mult, op1=ALU.add),
               reads=[up.buf, t2b, cstb], writes=[t3b])
            ge, geb = ge_rot.next()
            op("act", lambda: act.activation(ge[:], t3[:], AF.Gelu, bias=C("fcb", l * NCH + cg, l * NCH + cg + 1)),
               reads=[t3b, cstb], writes=[geb])
            g, gb = g_rot.next()
            op("dve", lambda: dve.tensor_tensor(g[:], ge[:], vp.ap(), op=ALU.mult), reads=[geb, vp.buf], writes=[gb])
            gq[idx] = (g, gb)

        def Dn(idx):
            j, i = items[idx]
            g, gb = gq.pop(idx)
            do = 2 * KC * n * 128
            for m in range(KC):
                op("pe", lambda: pe.matmul(H[m].ap(), W[:, do + i * 1024 + m * 128:do + i * 1024 + (m + 1) * 128], g[:],
                                           start=(i == 0 and m % 2 == 0), stop=(i == n - 1), skip_group_check=True),
                   reads=[wb, gb], writes=[H[m].buf], inc=(m == KC - 1))
            if i == n - 1:
                for m in range(KC):
                    resid_add(j, m, H[m])
                if on_subtile_done is not None:
                    on_subtile_done(j)

        LAG = 2
        for idx in range(len(items)):
            UV(idx)
            if idx >= LAG:
                Dn(idx - LAG)
        for idx in range(max(0, len(items) - LAG), len(items)):
            Dn(idx)

    wait_x = []
    for g in range(NG):
        t0 = g * G
        for j in range(NSUB):
            tk.dma("sp", "xin%d" % j, xres[:, :, tok(j)],
                   xT[:, t0 + j * SUB:t0 + (j + 1) * SUB].rearrange("(c p) t -> p c t", p=128),
                   writes=[xb[c][j] for c in range(KC)])
        out_done = set()

        def emit_output(j, t0=t0):
            if do_final:
                rstd, rstdb = norm_stats(j)
                for c in range(KC):
                    op("dve", lambda: dve.scalar_tensor_tensor(out=xres[:, c, tok(j)], in0=xres[:, c, tok(j)],
                                                                scalar=C("finn", c, c + 1), in1=rstd[:],
                                                                op0=ALU.mult, op1=ALU.mult),
                       reads=[xb[c][j], rstdb, cstb], writes=[xb[c][j]])
            tk.dma("sp", "out%d" % j,
                   yT[:, t0 + j * SUB:t0 + (j + 1) * SUB].rearrange("(c p) t -> p c t", p=128),
                   xres[:, :, tok(j)], reads=[xb[c][j] for c in range(KC)])
            out_done.add(j)

        for l in layers:
            if debug_stage == "load":
                break
            if debug_stage == "norm":
                for j in range(NSUB):
                    norm_to_hn(j, "mixn", l)
                break
            if debug_stage == "wload":
                s = begin_piece("w_m0")
                pt = half_rot.next()
                op("pe", lambda: pe.matmul(pt.ap(), slots[s][:, 0:128], slots[s][:, 128:384], start=True, stop=True),
                   reads=[slotb[s]], writes=[pt.buf])
                op("dve", lambda: dve.tensor_copy(sx_t[0][:], pt.ap()), reads=[pt.buf], writes=[sx_rot.items[0][1]])
                break
            if debug_stage in ("mix_proj", "mix_pool", "mix_sconv"):
                s = begin_piece("w_m0")
                norm_to_hn(0, "mixn", l)
                mixer0(0, s, first_tile=True, part=debug_stage[4:])
                break
            if l % 2 == 0:
                s = begin_piece("w_m0")
                norm_to_hn(0, "mixn", l)
                for j in range(NSUB):
                    if j + 1 < NSUB:
                        norm_to_hn(j + 1, "mixn", l)
                    mixer0(j, s, first_tile=(g == 0 and j == 0))
                    if debug_stage is None:
                        norm_to_hn(j, "ffnn", l)
                if debug_stage == "mixer":
                    break
            else:
                s_in = begin_piece("w_gin")
                s_out = begin_piece("w_gout", prefetch=False)
                norm_to_hn(0, "mixn", l)
                for j in range(NSUB):
                    if j + 1 < NSUB:
                        norm_to_hn(j + 1, "mixn", l)
                    mixer1(j, s_in, s_out, last=(j == NSUB - 1))
                if debug_stage == "mix1":
                    break
            op("act", lambda: act.copy(hn[:, :, 0:2], halo_save[:, l]), reads=[halob[l]], writes=[hnhalo])
            if debug_stage is not None or l % 2 == 1:
                for j in range(NSUB):
                    norm_to_hn(j, "ffnn", l)
            op("act", lambda: act.copy(halo_save[:, l], hn[:, :, G:G + 2]), reads=[hnb[NSUB - 1]], writes=[halob[l]])
            for pi in range(len(FFN_SPLIT)):
                s = begin_piece("w_f%d_%d" % (l, pi))
                is_tail = (l == layers[-1] and pi == len(FFN_SPLIT) - 1 and debug_stage is None)
                ffn_piece(l, pi, s, on_subtile_done=(emit_output if is_tail else None))
        for j in range(NSUB):
            if j in out_done:
                continue
            if do_final and debug_stage is None:
                rstd, rstdb = norm_stats(j)
                for c in range(KC):
                    op("dve", lambda: dve.scalar_tensor_tensor(out=xres[:, c, tok(j)], in0=xres[:, c, tok(j)],
                                                                scalar=C("finn", c, c + 1), in1=rstd[:],
                                                                op0=ALU.mult, op1=ALU.mult),
                       reads=[xb[c][j], rstdb, cstb], writes=[xb[c][j]])
            tk.dma("sp", "out%d" % j,
                   yT[:, t0 + j * SUB:t0 + (j + 1) * SUB].rearrange("(c p) t -> p c t", p=128),
                   xres[:, :, tok(j)], reads=[xb[c][j] for c in range(KC)])
    tk.wait_all("sp", [xb[c][j] for c in range(KC) for j in range(NSUB)])
    es.close()
    return nc


def _run(nc, xTs, wts, names):
    in_maps = []
    for c in range(N_CORES):
        m = {"xT": xTs[c], "cst": wts["cst"], "wg2": wts["wg2"]}
        for nme in names:
            m[nme] = wts[nme]
        in_maps.append(m)
    res = run_bass_kernel_spmd(nc, in_maps, core_ids=list(range(N_CORES)))
    return [r["yT"] for r in res.results]


def _piece_names(layers):
    names = []
    for l in layers:
        names += ["w_m0"] if l % 2 == 0 else ["w_gin", "w_gout"]
        names += ["w_f%d_%d" % (l, i) for i in range(len(FFN_SPLIT))]
    return names


def kernel(**inputs):
    x = np.asarray(inputs["x"], np.float32)
    wts = prepare_host(inputs)
    xTs = [np.ascontiguousarray(x[b].T) for b in range(N_CORES)]
    nc = build_program(layers=(0, 1), do_final=True)
    ys = _run(nc, xTs, wts, _piece_names((0, 1)))
    return np.stack([np.ascontiguousarray(y.T) for y in ys], axis=0).astype(np.float32)
```
